# Optimizing a Trainium2 kernel written in Bass

```python
import jax, jax.numpy as jnp
from jax import lax
import numpy as np

D_MODEL = 2048
BATCH = 2
SEQ = 4096
DEPTH = 4

N_MIXERS = 2
N_A_LAYERS = (DEPTH + N_MIXERS - 1) // N_MIXERS
N_B_LAYERS = DEPTH // N_MIXERS
HGRN_EXPAND = 128
HGRN_HEADS = D_MODEL // HGRN_EXPAND
HGRN_KEY_DIM = HGRN_HEADS * HGRN_EXPAND
HGRN_HEAD_V = D_MODEL // HGRN_HEADS
CHUNK = 64
CONV_WIDTH = 3
FFN_DIM = 5632
EPS = 1e-6

kernel_name = "bidir_hgrn2_shortconv_convffn_hybrid"


def rmsnorm(x, w):
    x32 = x.astype(jnp.float32)
    y = x32 * lax.rsqrt(jnp.mean(x32 * x32, axis=-1, keepdims=True) + EPS)
    return (y * w.astype(jnp.float32)).astype(x.dtype)


def dwconv3(z, w):
    L = z.shape[1]
    zp = jnp.pad(z, ((0, 0), (1, 1), (0, 0)))
    return w[0] * zp[:, :L] + w[1] * zp[:, 1:L + 1] + w[2] * zp[:, 2:]


def chunked_gated_recurrence(q, k, log_f, v):
    N, H, L, K = q.shape
    V = v.shape[-1]
    n_chunks = L // CHUNK

    def to_chunks(t):
        return t.reshape(N, H, n_chunks, CHUNK, t.shape[-1]).transpose(2, 0, 1, 3, 4)

    causal_in_chunk = jnp.tril(jnp.ones((CHUNK, CHUNK), dtype=bool))[:, :, None]

    def step(S, inp):
        qi, ki, gi, vi = inp
        b = jnp.cumsum(gi, axis=2)
        inter = jnp.einsum('nhtk,nhkv->nhtv', qi * jnp.exp(b), S)
        diff = b[:, :, :, None, :] - b[:, :, None, :, :]
        decay = jnp.exp(jnp.where(causal_in_chunk, diff, -jnp.inf))
        scores = jnp.einsum('nhtk,nhsk,nhtsk->nhts', qi, ki, decay)
        intra = jnp.einsum('nhts,nhsv->nhtv', scores, vi)
        b_last = b[:, :, -1]
        k_dec = ki * jnp.exp(b_last[:, :, None, :] - b)
        S_new = jnp.exp(b_last)[..., None] * S + jnp.einsum('nhck,nhcv->nhkv', k_dec, vi)
        return S_new, inter + intra

    S0 = jnp.zeros((N, H, K, V), jnp.float32)
    _, o = lax.scan(step, S0, (to_chunks(q), to_chunks(k), to_chunks(log_f), to_chunks(v)))
    return o.transpose(1, 2, 0, 3, 4).reshape(N, H, L, V)


def hgrn2_mixer(h, w_in, w_out, g_norm_w, lb_fwd, lb_bwd):
    B_, L, _ = h.shape
    K, D = HGRN_KEY_DIM, D_MODEL
    u = h @ w_in
    q, f_f, f_b, v, g = jnp.split(u, [K, 2 * K, 3 * K, 3 * K + D], axis=-1)
    q = jax.nn.silu(q)

    def gates(f_raw, lb):
        f = lb + (1.0 - lb) * jax.nn.sigmoid(f_raw.astype(jnp.float32))
        return 1.0 - f, jnp.log(f)

    k_f, lf_f = gates(f_f, lb_fwd)
    k_b, lf_b = gates(f_b, lb_bwd)

    def heads(t, d):
        return t.reshape(t.shape[0], L, HGRN_HEADS, d).transpose(0, 2, 1, 3).astype(jnp.float32)

    rev = lambda t: t[:, ::-1]
    qs = heads(jnp.concatenate([q, rev(q)], axis=0), HGRN_EXPAND)
    ks = heads(jnp.concatenate([k_f, rev(k_b)], axis=0), HGRN_EXPAND)
    ls = heads(jnp.concatenate([lf_f, rev(lf_b)], axis=0), HGRN_EXPAND)
    vs = heads(jnp.concatenate([v, rev(v)], axis=0), HGRN_HEAD_V)
    o = chunked_gated_recurrence(qs, ks, ls, vs)
    o = o[:B_] + o[B_:, :, ::-1]
    o = o.transpose(0, 2, 1, 3)
    gh = g.reshape(B_, L, HGRN_HEADS, HGRN_HEAD_V)
    o = rmsnorm(o, g_norm_w) * jax.nn.silu(gh.astype(jnp.float32))
    return o.reshape(B_, L, D).astype(h.dtype) @ w_out


def short_conv_mixer(h, w_in, w_conv, w_out):
    u = h @ w_in
    gate_b, gate_c, x_in = jnp.split(u, 3, axis=-1)
    z = dwconv3(gate_c * x_in, w_conv)
    return (gate_b * z) @ w_out


def conv_ffn(h, w_in, w_conv, w_out):
    up = dwconv3(h @ w_in, w_conv)
    gate, val = jnp.split(up, 2, axis=-1)
    return (jax.nn.silu(gate) * val) @ w_out


def setup_inputs(seed: int = 0) -> dict:
    key = jax.random.key(seed)
    ks = jax.random.split(key, 16)
    D, K, F = D_MODEL, HGRN_KEY_DIM, FFN_DIM
    nrm = lambda k, shape, scale: jax.random.normal(k, shape, jnp.float32) * scale
    return {
        "x": nrm(ks[0], (BATCH, SEQ, D), 1.0),
        "hgrn_w_in": nrm(ks[1], (N_A_LAYERS, D, 3 * K + 2 * D), D ** -0.5),
        "hgrn_w_out": nrm(ks[2], (N_A_LAYERS, D, D), D ** -0.5),
        "hgrn_gnorm": 1.0 + nrm(ks[3], (N_A_LAYERS, HGRN_HEAD_V), 0.02),
        "hgrn_lower_bounds": nrm(ks[4], (2, N_A_LAYERS, K), 1.0),
        "sconv_w_in": nrm(ks[5], (N_B_LAYERS, D, 3 * D), D ** -0.5),
        "sconv_w_conv": nrm(ks[6], (N_B_LAYERS, CONV_WIDTH, D), CONV_WIDTH ** -0.5),
        "sconv_w_out": nrm(ks[7], (N_B_LAYERS, D, D), D ** -0.5),
        "ffn_w_in": nrm(ks[8], (DEPTH, D, 2 * F), D ** -0.5),
        "ffn_w_conv": nrm(ks[9], (DEPTH, CONV_WIDTH, 2 * F), CONV_WIDTH ** -0.5),
        "ffn_w_out": nrm(ks[10], (DEPTH, F, D), F ** -0.5),
        "norm_pre_mix": 1.0 + nrm(ks[11], (DEPTH, D), 0.02),
        "norm_post_mix": 1.0 + nrm(ks[12], (DEPTH, D), 0.02),
        "norm_pre_ffn": 1.0 + nrm(ks[13], (DEPTH, D), 0.02),
        "norm_post_ffn": 1.0 + nrm(ks[14], (DEPTH, D), 0.02),
    }


def reference(x, hgrn_w_in, hgrn_w_out, hgrn_gnorm, hgrn_lower_bounds,
              sconv_w_in, sconv_w_conv, sconv_w_out,
              ffn_w_in, ffn_w_conv, ffn_w_out,
              norm_pre_mix, norm_post_mix, norm_pre_ffn, norm_post_ffn):
    lb = jnp.cumsum(jax.nn.softmax(hgrn_lower_bounds.astype(jnp.float32), axis=1), axis=1)
    lb = lb - lb[:, :1]
    for i in range(DEPTH):
        j = i // N_MIXERS
        h = rmsnorm(x, norm_pre_mix[i])
        if i % N_MIXERS == 0:
            m = hgrn2_mixer(h, hgrn_w_in[j], hgrn_w_out[j], hgrn_gnorm[j], lb[0, j], lb[1, j])
        else:
            m = short_conv_mixer(h, sconv_w_in[j], sconv_w_conv[j], sconv_w_out[j])
        x = x + rmsnorm(m, norm_post_mix[i])
        h = rmsnorm(x, norm_pre_ffn[i])
        x = x + rmsnorm(conv_ffn(h, ffn_w_in[i], ffn_w_conv[i], ffn_w_out[i]), norm_post_ffn[i])
    return x
```

```python
import numpy as np
import os
import concourse.bass as bass
import concourse.mybir as mybir
from concourse.bass_utils import run_bass_kernel_spmd

F32 = mybir.dt.float32
BF16 = mybir.dt.bfloat16
AF = mybir.ActivationFunctionType
ALU = mybir.AluOpType

P = 128
D = 2048
DC = 16
T = 1024
F = 5632
FC = 44
NCORE = 8
EPS = 1e-6
NSLOT = 4
SROW = 130
FULL_PLAN = [("hgrn", 0), ("ffn", 0), ("sconv", 0), ("ffn", 1), ("hgrn", 1), ("ffn", 2), ("sconv", 1), ("ffn", 3)]


class Op:
    __slots__ = ("eng", "fn", "deps", "dma_key", "sig", "sem", "val")


class Sched:
    def __init__(self, nc):
        self.nc = nc
        self.plan = False
        self.ops = []
        self.last_w = {}
        self.readers = {}
        self.last_eng = {}
        self.sp_dmas = []
        self.bar = {}

    def barrier(self):
        if self.plan:
            return
        s = set(self.last_eng[e] for e in ("pe", "act", "dve") if e in self.last_eng)
        s |= set(self.sp_dmas)
        self.sp_dmas = []
        for e in ("pe", "act", "dve", "sp"):
            self.bar[e] = set(s) | self.bar.get(e, set())

    def add(self, eng, fn, reads=(), writes=(), dma_key=None):
        if self.plan:
            return None
        op = Op()
        op.eng, op.fn, op.dma_key, op.sig = eng, fn, dma_key, False
        deps = set()
        for k in reads:
            w = self.last_w.get(k)
            if w is not None:
                deps.add(w)
        for k in writes:
            w = self.last_w.get(k)
            if w is not None:
                deps.add(w)
            for r in self.readers.get(k, ()):
                deps.add(r)
        for k in writes:
            self.last_w[k] = op
            self.readers[k] = []
        for k in reads:
            if self.last_w.get(k) is not op:
                self.readers.setdefault(k, []).append(op)
        if self.bar.get(eng):
            deps |= self.bar[eng]
            self.bar[eng] = set()
        deps.discard(op)
        if eng == "pe":
            deps = {d for d in deps if d.eng != "pe"}
        op.deps = deps
        self.ops.append(op)
        if eng != "pool":
            self.last_eng[eng] = op
        if eng == "sp":
            self.sp_dmas.append(op)
        return op

    def emit(self, final_keys=()):
        nc = self.nc
        for op in self.ops:
            for d in op.deps:
                d.sig = True
            if op.dma_key:
                op.sig = True
        sems, cnt = {}, {}
        for op in self.ops:
            if not op.sig:
                continue
            key = ("dma", op.dma_key) if op.dma_key else ("eng", op.eng)
            if key not in sems:
                sems[key] = nc.alloc_semaphore("s%d" % len(sems))
                cnt[key] = 0
            op.sem = sems[key]
            cnt[key] += 16 if op.dma_key else 1
            op.val = cnt[key]
        by_eng = {}
        for op in self.ops:
            by_eng.setdefault(op.eng, []).append(op)
        finals = [(sems[("dma", k)], cnt[("dma", k)]) for k in final_keys]

        def run(engname, eng):
            waited = {}
            for op in by_eng.get(engname, []):
                need = {}
                for d in op.deps:
                    if need.get(d.sem, 0) < d.val:
                        need[d.sem] = d.val
                for s_, v in need.items():
                    if waited.get(s_, 0) < v:
                        eng.wait_ge(s_, v)
                        waited[s_] = v
                inst = op.fn(eng)
                if op.sig:
                    inst.then_inc(op.sem, 16 if op.dma_key else 1)
            if engname == "sp":
                for s_, v in finals:
                    eng.wait_ge(s_, v)

        with nc.Block() as block:
            @block.tensor
            def _(e):
                run("pe", e)

            @block.scalar
            def _(e):
                run("act", e)

            @block.vector
            def _(e):
                run("dve", e)

            @block.gpsimd
            def _(e):
                run("pool", e)

            @block.sync
            def _(e):
                run("sp", e)


class WStream:
    def __init__(self, S, slots):
        self.S, self.slots = S, slots
        self.units = []
        self.issued = 0
        self.pos = 0

    def next(self, src):
        if self.S.plan:
            self.units.append(src)
            return 0, self.slots[0]
        i = self.pos
        self.pos += 1
        hi = min(i + NSLOT, len(self.units))
        while self.issued < hi:
            u = self.issued
            sl = u % NSLOT
            dst, s_ap = self.slots[sl], self.units[u]
            self.S.add("pool", lambda e, dst=dst, s_ap=s_ap: e.dma_start(out=dst[:, :], in_=s_ap),
                       writes=[("w", sl)], dma_key="w%d" % sl)
            self.issued += 1
        return i % NSLOT, self.slots[i % NSLOT]


def build(plan):
    nc = bass.Bass("TRN2", target_bir_lowering=False)
    dram = {}

    def din(name, shape):
        dram[name] = nc.dram_tensor(name, list(shape), F32, kind="ExternalInput")
        return dram[name]

    xT_d = din("xT", [D, T])
    wshapes = {"hgrn_win": [80, P, 2048], "hgrn_wout": [D, D], "sconv_win": [48, P, 2048], "sconv_wout": [D, D],
               "ffn_win": [88, P, 2048], "ffn_wout": [F, D]}

    def W(name, l):
        key = "%s_%d" % (name, l)
        if key not in dram:
            din(key, wshapes[name])
        return dram[key].ap()

    normw_d = din("normw", [P, 256])
    fconv_d = din("fconv", [P, 4 * 3 * 88])
    sconvw_d = din("sconvw", [P, 2 * 3 * 16])
    gnorm_d = din("gnorm", [P, 2])
    lbraw_d = din("lbraw", [P, 64])
    cmask_d = din("cmask", [P, 32])
    out_d = nc.dram_tensor("outT", [D, T], F32, kind="ExternalOutput")
    agh_in = nc.dram_tensor("agh_in", [16, 256], F32)
    agh_out = nc.dram_tensor("agh_out", [16 * NCORE, 256], F32)
    SCOLS = 16 * 2 * SROW
    ags_in = nc.dram_tensor("ags_in", [P, SCOLS], F32)
    ags_out = nc.dram_tensor("ags_out", [P * NCORE, SCOLS], F32)

    def sb(name, shape, dt):
        return nc.alloc_sbuf_tensor("sb_" + name, shape, dt)
    x = sb("x", [P, DC, T], F32)
    hT = sb("hT", [P, DC, T + 2], BF16)
    actbuf = sb("actbuf", [P, 44 * 512], BF16)
    wslots = [sb("wslot%d" % i, [P, 2048], BF16) for i in range(NSLOT)]
    normw = sb("normw", [P, 4, 4, DC], F32)
    fconv = sb("fconv", [P, 4, 3, 88], F32)
    sconvw = sb("sconvw", [P, 2, 3, 16], F32)
    gnorm = sb("gnorm", [P, 2], F32)
    lbraw = sb("lbraw", [P, 2, 2, 16], F32)
    cmask = sb("cmask", [P, 32], F32)
    omm = sb("omm", [P, 32], F32)
    lbv = sb("lbv", [P, 2, 2, 16], F32)
    oml = sb("oml", [P, 2, 2, 16], F32)
    ones_b = sb("ones_b", [P, P], BF16)
    ones_h = sb("ones_h", [P, P], BF16)
    ident = sb("ident", [P, P], BF16)
    identf = sb("identf", [P, P], F32)
    maskf = sb("maskf", [P, 2, P], F32)
    onesf = sb("onesf", [P, T], F32)
    xh = sb("xh", [P, DC, 2], F32)
    SCR = 9664
    scr = sb("scr", [P, SCR], F32)
    pb = [nc.alloc_psum_tensor("pb%d" % i, [P, 512], F32) for i in range(8)]

    def carve(off, n, dt=F32):
        a = scr[:, off:off + n]
        return a if dt == F32 else a.bitcast(BF16)

    S = Sched(nc)
    ws = WStream(S, wslots)

    def mm(out, lhsT, rhs, start, stop, reads, writes):
        S.add("pe", lambda e: e.matmul(out, lhsT=lhsT, rhs=rhs, start=start, stop=stop, skip_group_check=True),
              reads, writes)

    def tr(out, in_, idn, reads, writes):
        S.add("pe", lambda e: e.transpose(out, in_, idn), reads, writes)

    def px(writes, *aps):
        for a in aps:
            try:
                if a.name.startswith("pb"):
                    return list(writes) + ["psum_rd"]
            except AttributeError:
                pass
        return writes

    def act(out, in_, func, reads, writes, bias=None, scale=None):
        writes = px(writes, in_)
        kw = {}
        if bias is not None:
            kw["bias"] = bias
        if scale is not None:
            kw["scale"] = scale
        S.add("act", lambda e: e.activation(out=out, in_=in_, func=func, **kw), reads, writes)

    def tt(out, in0, in1, op, reads, writes, eng="dve"):
        writes = px(writes, in0, in1)
        S.add(eng, lambda e: e.tensor_tensor(out=out, in0=in0, in1=in1, op=op), reads, writes)

    def ts(out, in0, s1, s2, op0, op1, reads, writes, eng="dve"):
        writes = px(writes, in0)
        if op1 is None:
            S.add(eng, lambda e: e.tensor_scalar(out=out, in0=in0, scalar1=s1, scalar2=None, op0=op0), reads, writes)
        else:
            S.add(eng, lambda e: e.tensor_scalar(out=out, in0=in0, scalar1=s1, scalar2=s2, op0=op0, op1=op1),
                  reads, writes)

    def stt(out, in0, scalar, in1, op0, op1, reads, writes, eng="dve"):
        writes = px(writes, in0, in1)
        S.add(eng, lambda e: e.scalar_tensor_tensor(out=out, in0=in0, scalar=scalar, in1=in1, op0=op0, op1=op1),
              reads, writes)

    def rsqrt(out, in_, reads, wkey):
        S.add("act", lambda e: e.activation(out=out, in_=in_, func=AF.Sqrt, bias=EPS), reads, [wkey, "psum_rd"] + list(reads))
        S.add("dve", lambda e: e.reciprocal(out=out, in_=out), [wkey], [wkey])

    def cp(out, in_, reads, writes, eng="dve"):
        writes = px(writes, in_)
        if eng == "act":
            S.add(eng, lambda e: e.activation(out=out, in_=in_, func=AF.Copy), reads, writes)
        else:
            S.add(eng, lambda e: e.tensor_copy(out=out, in_=in_), reads, writes)

    def dma(out, in_, reads, writes, key, eng="sp"):
        S.add(eng, lambda e: e.dma_start(out=out, in_=in_), reads, writes, dma_key=key)

    def memset(ap, val, writes, eng="dve"):
        S.add(eng, lambda e: e.memset(ap, val), (), writes)

    def setup():
        xv = xT_d.ap().rearrange("(c p) t -> p c t", p=P)
        for c4 in range(4):
            dma(x[:, c4 * 4:(c4 + 1) * 4, :], xv[:, c4 * 4:(c4 + 1) * 4, :], (), [("x", c) for c in range(c4 * 4, c4 * 4 + 4)], "ld")
        dma(normw[:].rearrange("p a b c -> p (a b c)"), normw_d.ap(), (), ["normw"], "ld")
        dma(fconv[:].rearrange("p a b c -> p (a b c)"), fconv_d.ap(), (), ["fconv"], "ld")
        dma(sconvw[:].rearrange("p a b c -> p (a b c)"), sconvw_d.ap(), (), ["sconvw"], "ld")
        dma(gnorm[:, :], gnorm_d.ap(), (), ["gnorm"], "ld")
        dma(lbraw[:].rearrange("p a b c -> p (a b c)"), lbraw_d.ap(), (), ["lbraw"], "ld")
        dma(cmask[:, :], cmask_d.ap(), (), ["cmask"], "ld")
        memset(ones_b[:, :], 1.0 / D, ["ones_b"])
        memset(ones_h[:, :], 1.0 / P, ["ones_h"])
        memset(onesf[:, :], 1.0, ["onesf"])
        memset(identf[:, :], 1.0, ["identf"])
        memset(maskf[:, :, :], 1.0, ["maskf"])
        memset(xh[:, :, :], 0.0, ["xh"])
        S.add("pool", lambda e: e.affine_select(out=identf[:, :], in_=identf[:, :], pattern=[[-1, P]],
                                                compare_op=ALU.is_equal, fill=0.0, base=0, channel_multiplier=1),
              ["identf"], ["identf"])
        cp(ident[:, :], identf[:, :], ["identf"], ["ident"], eng="act")
        for dr in range(2):
            for blk in range(2):
                sl = maskf[:, dr, blk * 64:(blk + 1) * 64]
                S.add("pool", lambda e, sl=sl, blk=blk: e.affine_select(
                    out=sl, in_=sl, pattern=[[0, 64]], compare_op=ALU.is_ge, fill=0.0,
                    base=-blk * 64, channel_multiplier=1), ["maskf"], ["maskf"])
                S.add("pool", lambda e, sl=sl, blk=blk: e.affine_select(
                    out=sl, in_=sl, pattern=[[0, 64]], compare_op=ALU.is_ge, fill=0.0,
                    base=blk * 64 + 63, channel_multiplier=-1), ["maskf"], ["maskf"])
                if dr == 0:
                    S.add("pool", lambda e, sl=sl, blk=blk: e.affine_select(
                        out=sl, in_=sl, pattern=[[1, 64]], compare_op=ALU.is_ge, fill=0.0,
                        base=blk * 64, channel_multiplier=-1), ["maskf"], ["maskf"])
                else:
                    S.add("pool", lambda e, sl=sl, blk=blk: e.affine_select(
                        out=sl, in_=sl, pattern=[[-1, 64]], compare_op=ALU.is_ge, fill=0.0,
                        base=-blk * 64, channel_multiplier=1), ["maskf"], ["maskf"])
        ts(omm[:, :], cmask[:, :], -1.0, 1.0, ALU.mult, ALU.add, ["cmask"], ["omm"])
        memset(lbv[:, :, 0, :], 0.0, ["lbv"])
        tt(lbv[:, :, 1, :], lbraw[:, :, 1, :], lbraw[:, :, 0, :], ALU.subtract, ["lbraw", "lbv"], ["lbv"])
        act(lbv[:, :, 1, :], lbv[:, :, 1, :], AF.Sigmoid, ["lbv"], ["lbv"])
        ts(oml[:].rearrange("p a b c -> p (a b c)"), lbv[:].rearrange("p a b c -> p (a b c)"), -1.0, 1.0,
           ALU.mult, ALU.add, ["lbv"], ["oml"])
        S.barrier()

    def halo_exchange():
        hs = carve(0, 32)
        G = carve(64, NCORE * 32).rearrange("p (r f) -> p r f", r=NCORE)
        xk = [("x", c) for c in range(DC)]
        cp(hs[:, 0:16], x[:, :, 0], xk, ["hs"], eng="act")
        cp(hs[:, 16:32], x[:, :, T - 1], xk, ["hs"], eng="act")
        dma(agh_in.ap().rearrange("a (b f) -> (a b) f", f=32), hs, ["hs"], ["agh_in"], "hx1")
        if not os.environ.get("DBG_NOCC"):
            S.add("pool", lambda e: e.collective_compute("AllGather", ALU.bypass, replica_groups=[list(range(NCORE))],
                                                          ins=[agh_in.ap().opt()], outs=[agh_out.ap().opt()]),
                  ["agh_in"], ["agh_out"])
        dma(G, bass.AP(agh_out, 0, [[32, P], [4096, NCORE], [1, 32]]), ["agh_out"], ["G"], "hx2")
        for side in range(2):
            for r in range(NCORE):
                src = G[:, r, 16:32] if side == 0 else G[:, r, 0:16]
                sel = cmask[:, side * 8 + r: side * 8 + r + 1]
                if r == 0:
                    ts(xh[:, :, side], src, sel, None, ALU.mult, None, ["G", "cmask"], ["xh"])
                else:
                    stt(xh[:, :, side], src, sel, xh[:, :, side], ALU.mult, ALU.add, ["G", "cmask", "xh"], ["xh"])
        S.barrier()

    def prenorm(kind, layer, halo):
        sq = [carve(0, 512, BF16), carve(512, 512, BF16)]
        sqh = carve(1024, 16, BF16).rearrange("p (c w) -> p c w", w=2)
        rstd = carve(1040, T + 2)
        if halo:
            act(sqh, xh[:, :, :], AF.Square, ["xh"], ["sqh"])
        for c in range(DC):
            b = c % 2
            act(sq[b][:, :], x[:, c, :], AF.Square, [("x", c)], [("sq", b)])
            for h in range(2):
                mm(pb[h][:, :], ones_b[:, :], sq[b][:, h * 512:(h + 1) * 512], c == 0, c == DC - 1,
                   [("sq", b), "ones_b"], [("pb", h)])
            if halo:
                mm(pb[2][:, 0:2], ones_b[:, :], sqh[:, c, :], c == 0, c == DC - 1, ["sqh", "ones_b"], [("pb", 2)])
        for h in range(2):
            rsqrt(rstd[:, 1 + h * 512: 1 + (h + 1) * 512], pb[h][:, :], [("pb", h)], ("rstd", h))
        if halo:
            rsqrt(rstd[:, 0:T + 2:T + 1], pb[2][:, 0:2], [("pb", 2)], "rstdh")
        for c in range(DC):
            wcol = normw[:, kind, layer, c:c + 1]
            stt(hT[:, c, 1:T + 1], x[:, c, :], wcol, rstd[:, 1:T + 1], ALU.mult, ALU.mult,
                [("x", c), "normw", ("rstd", 0), ("rstd", 1)], [("h", c)])
        hk = [("h", c) for c in range(DC)]
        if halo:
            tmp = carve(1040 + T + 2 + 6, 32).rearrange("p (c w) -> p c w", w=2)
            tt(tmp, xh[:, :, :], rstd[:, 0:T + 2:T + 1].unsqueeze(1).broadcast_to([P, DC, 2]), ALU.mult,
               ["xh", "rstdh"], ["htmp"])
            tt(hT[:, :, 0:T + 2:T + 1], tmp, normw[:, kind, layer, :].unsqueeze(2).broadcast_to([P, DC, 2]), ALU.mult,
               ["htmp", "normw"] + hk, hk)
        S.barrier()

    def outproj(KC, aT, tok0, atok0, wsrc, kind, layer):
        m_sb = carve(0, 4096).rearrange("p (c t) -> p c t", c=DC)
        sq = [carve(4096, 256, BF16), carve(4352, 256, BF16)]
        rstd = carve(4608, 256)
        for kc in range(KC):
            sl, w = ws.next(wsrc(kc))
            for dch in range(DC):
                mm(pb[dch // 2][:, (dch % 2) * 256:(dch % 2 + 1) * 256], w[:, dch * P:(dch + 1) * P],
                   aT[:, kc, atok0:atok0 + 256], kc == 0 and dch % 2 == 0, kc == KC - 1,
                   [("w", sl), ("a", kc)], [("pb", dch // 2)])
        LV = 9
        for i in range(8):
            cp(m_sb[:, 2 * i:2 * i + 2, :], pb[i][:, :].rearrange("p (c t) -> p c t", c=2), [("pb", i)], [("m", i), ("pb", i)])
            act(sq[i % 2][:, :], m_sb[:, 2 * i:2 * i + 2, :], AF.Square, [("m", i)], [("sq", i % 2)])
            for h in range(2):
                mm(pb[0][:, 0:256], ones_b[:, :], sq[i % 2][:, h * 256:(h + 1) * 256], i == 0 and h == 0,
                   i == 7 and h == 1, [("sq", i % 2), "ones_b"], [("pb", 0)])
        mk = [("m", i) for i in range(8)]
        if LV >= 4:
            rsqrt(rstd[:, :], pb[0][:, 0:256], [("pb", 0)], "rstd")
        if LV >= 5:
            tt(m_sb[:, :, :], m_sb[:, :, :], rstd[:, :].unsqueeze(1).broadcast_to([P, DC, 256]), ALU.mult,
               mk + ["rstd"], mk)
        if LV >= 6:
            for dch in range(DC):
                xs = x[:, dch, tok0:tok0 + 256]
                stt(xs, m_sb[:, dch, :], normw[:, kind, layer, dch:dch + 1], xs, ALU.mult, ALU.add,
                    mk + ["normw", ("x", dch)], [("x", dch)])
        S.barrier()

    def ffn(layer):
        halo_exchange()
        prenorm(2, layer, True)
        a = actbuf[:, :].rearrange("p (j t) -> p j t", j=FC)
        ug = [carve(0, 514), carve(514, 514)]
        uv = [carve(1028, 514), carve(1542, 514)]
        cg = [carve(2056, 512), carve(2568, 512)]
        cv = [carve(3080, 512), carve(3592, 512)]
        sg = [carve(4104, 512), carve(4616, 512)]
        hk = [("h", c) for c in range(DC)]
        for hf in range(2):
            win0 = hf * 512
            for j in range(int(os.environ.get('DBG_PAIRS', FC))):
                b = j % 2
                for which, (u_, c_) in enumerate(((ug, cg), (uv, cv))):
                    unit = j + which * FC
                    sl, w = ws.next(W("ffn_win", layer)[unit])
                    wv = w[:, :].rearrange("p (k c) -> p k c", k=DC)
                    bk = b * 4 + which * 2
                    for k in range(DC):
                        for pc in range(2):
                            mm(pb[bk + pc][:, 0:257], wv[:, k, :], hT[:, k, win0 + pc * 257: win0 + (pc + 1) * 257],
                               k == 0, k == DC - 1, [("w", sl), ("h", k)], [("pb", bk + pc)])
                    for pc in range(2):
                        cp(u_[b][:, pc * 257:(pc + 1) * 257], pb[bk + pc][:, 0:257], [("pb", bk + pc)],
                           [("u", which, b), ("pb", bk + pc)], eng="act")
                    cw = fconv[:, layer, :, unit]
                    ts(c_[b][:, :], u_[b][:, 1:513], cw[:, 1:2], None, ALU.mult, None, [("u", which, b), "fconv"],
                       [("c", which, b)])
                    stt(c_[b][:, :], u_[b][:, 0:512], cw[:, 0:1], c_[b][:, :], ALU.mult, ALU.add,
                        [("u", which, b), "fconv", ("c", which, b)], [("c", which, b)])
                    stt(c_[b][:, :], u_[b][:, 2:514], cw[:, 2:3], c_[b][:, :], ALU.mult, ALU.add,
                        [("u", which, b), "fconv", ("c", which, b)], [("c", which, b)])
                act(sg[b][:, :], cg[b][:, :], AF.Silu, [("c", 0, b)], [("sg", b)])
                tt(a[:, j, :], sg[b][:, :], cv[b][:, :], ALU.mult, [("sg", b), ("c", 1, b)], [("a", j)])
            S.barrier()
            for sub in range(int(os.environ.get('DBG_SUBS', 2))):
                outproj(FC, a, hf * 512 + sub * 256, sub * 256,
                        lambda kc: W("ffn_wout", layer)[kc * P:(kc + 1) * P, :], 3, layer)

    def sconv(j, layer):
        halo_exchange()
        prenorm(0, layer, True)
        a = actbuf[:, 0:DC * 512].rearrange("p (j t) -> p j t", j=DC)
        c_sb = [carve(0, 514), carve(514, 514)]
        x_sb = [carve(1028, 514), carve(1542, 514)]
        pp = [carve(2056, 514), carve(2570, 514)]
        z = [carve(3084, 512), carve(3596, 512)]
        for hf in range(2):
            win0 = hf * 512
            for ch in range(DC):
                b = ch % 2
                for which, dst in ((1, c_sb), (2, x_sb)):
                    sl, w = ws.next(W("sconv_win", j)[which * DC + ch])
                    wv = w[:, :].rearrange("p (k c) -> p k c", k=DC)
                    bk = (which - 1) * 2
                    for k in range(DC):
                        for pc in range(2):
                            mm(pb[bk + pc][:, 0:257], wv[:, k, :], hT[:, k, win0 + pc * 257: win0 + (pc + 1) * 257],
                               k == 0, k == DC - 1, [("w", sl), ("h", k)], [("pb", bk + pc)])
                    for pc in range(2):
                        cp(dst[b][:, pc * 257:(pc + 1) * 257], pb[bk + pc][:, 0:257], [("pb", bk + pc)],
                           [("cs", which, b), ("pb", bk + pc)], eng="act")
                sl, w = ws.next(W("sconv_win", j)[ch])
                wv = w[:, :].rearrange("p (k c) -> p k c", k=DC)
                for k in range(DC):
                    mm(pb[4 + b][:, :], wv[:, k, :], hT[:, k, win0 + 1: win0 + 513], k == 0, k == DC - 1,
                       [("w", sl), ("h", k)], [("pb", 4 + b)])
                tt(pp[b][:, :], c_sb[b][:, :], x_sb[b][:, :], ALU.mult, [("cs", 1, b), ("cs", 2, b)], [("pp", b)])
                cw = sconvw[:, j, :, ch]
                ts(z[b][:, :], pp[b][:, 1:513], cw[:, 1:2], None, ALU.mult, None, [("pp", b), "sconvw"], [("z", b)])
                stt(z[b][:, :], pp[b][:, 0:512], cw[:, 0:1], z[b][:, :], ALU.mult, ALU.add,
                    [("pp", b), "sconvw", ("z", b)], [("z", b)])
                stt(z[b][:, :], pp[b][:, 2:514], cw[:, 2:3], z[b][:, :], ALU.mult, ALU.add,
                    [("pp", b), "sconvw", ("z", b)], [("z", b)])
                tt(a[:, ch, :], pb[4 + b][:, :], z[b][:, :], ALU.mult, [("z", b), ("pb", 4 + b)], [("a", ch), ("pb", 4 + b)])
            S.barrier()
            for sub in range(2):
                outproj(DC, a, hf * 512 + sub * 256, sub * 256,
                        lambda kc: W("sconv_wout", j)[kc * P:(kc + 1) * P, :], 1, layer)

    def hgrn(j, layer):
        prenorm(0, layer, False)
        oT = actbuf[:, 0:DC * T].rearrange("p (h t) -> p h t", h=DC)
        ab2 = actbuf[:, DC * T:]
        Sb = ab2[:, 0:4096].rearrange("p (d c v) -> p d c v", d=2, c=16)
        kT = ab2[:, 4096:6144].rearrange("p (d b k) -> p d b k", d=2, b=8)
        A_ = carve(0, 1024)
        B_ = carve(1024, 1024)
        C_ = carve(2112, 1024)
        Lg = carve(1024, NCORE * 2 * SROW).rearrange("p (r d f) -> p r d f", r=NCORE, d=2)
        qs = carve(3200, 1024)
        v_bf = carve(4224, 512, BF16)
        vT = carve(4736, 512, BF16).rearrange("p (b v) -> p b v", b=8)
        kt = carve(5248, 1024, BF16).rearrange("p (d t) -> p d t", d=2)
        qt = carve(6272, 1024, BF16).rearrange("p (d t) -> p d t", d=2)
        gs = carve(7296, 512, BF16)
        PT = carve(7808, 512, BF16).rearrange("p (b t) -> p b t", b=8)
        sqo = carve(8320, 512, BF16)
        Dc = carve(8832, 32).rearrange("p (d c) -> p d c", d=2)
        base = carve(8864, 16)
        KD = [carve(8880, 128), carve(9008, 128)]
        Sst = carve(9136, 256).rearrange("p (d v) -> p d v", d=2)
        acol = carve(9392, 2)
        stg = [carve(9396, 130), carve(9526, 130)]
        hk = [("h", c) for c in range(DC)]
        trb = pb[4][:, 0:256].bitcast(BF16)
        kvps = [pb[2][:, 0:P], pb[3][:, 0:P]]
        scp = pb[5]
        ob = [pb[6], pb[7]]

        def proj(seg, hd, bankpair):
            bankpair = 0
            sl, w = ws.next(W("hgrn_win", j)[seg * 16 + hd])
            wv = w[:, :].rearrange("p (k c) -> p k c", k=DC)
            for k in range(DC):
                for h in range(2):
                    mm(pb[bankpair * 2 + h][:, :], wv[:, k, :], hT[:, k, 1 + h * 512: 1 + (h + 1) * 512],
                       k == 0, k == DC - 1, [("w", sl), ("h", k)], [("pb", bankpair * 2 + h)])

        def transposes(src, dstfn, rk, wk):
            for g4 in range(2):
                for bi in range(4):
                    blk = g4 * 4 + bi
                    tr(trb[:, bi * P:(bi + 1) * P], src[:, blk * P:(blk + 1) * P], ident[:, :], rk + ["ident"], [("pb", 4)])
                cp(dstfn(g4), trb[:, :].rearrange("p (b v) -> p b v", b=4), [("pb", 4)], wk + [("pb", 4)], eng="act")

        for phase in range(2):
            for hd in range(DC):
                proj(3, hd, 0)
                for h in range(2):
                    cp(v_bf[:, h * 512:(h + 1) * 512], pb[h][:, :], [("pb", h)], ["v_bf", ("pb", h)], eng="act")
                transposes(v_bf, lambda g4: vT[:, g4 * 4:(g4 + 1) * 4, :], ["v_bf"], ["vT"])
                if phase == 1:
                    dma(Lg, bass.AP(ags_out, hd * 2 * SROW, [[SCOLS, P], [P * SCOLS, NCORE], [1, 2 * SROW]]).rearrange(
                        "p r (d f) -> p r d f", d=2), ["ags_out"], ["Lg", "B", "C"], "lg")
                    for dr in range(2):
                        memset(Sst[:, dr, :], 0.0, [("Sst", dr)])
                        order = range(NCORE) if dr == 0 else range(NCORE - 1, -1, -1)
                        for r in order:
                            mcol = cmask[:, 16 + dr * 8 + r: 16 + dr * 8 + r + 1]
                            ocol = omm[:, 16 + dr * 8 + r: 16 + dr * 8 + r + 1]
                            ts(acol[:, dr:dr + 1], Lg[:, r, dr, 128:129], mcol, ocol, ALU.mult, ALU.add,
                               ["Lg", "B", "C", "cmask", "omm"], [("acol", dr)])
                            ts(KD[dr][:, :], Lg[:, r, dr, 0:128], mcol, None, ALU.mult, None, ["Lg", "B", "C", "cmask"],
                               [("KD", dr)])
                            stt(Sst[:, dr, :], Sst[:, dr, :], acol[:, dr:dr + 1], KD[dr][:, :], ALU.mult, ALU.add,
                                [("Sst", dr), ("acol", dr), ("KD", dr)], [("Sst", dr)])
                    proj(0, hd, 1)
                    for h in range(2):
                        act(qs[:, h * 512:(h + 1) * 512], pb[h][:, :], AF.Silu, [("pb", h)], ["qs", ("pb", h)])
                for dr in range(2):
                    proj(1 + dr, hd, 0)
                    for h in range(2):
                        act(A_[:, h * 512:(h + 1) * 512], pb[h][:, :], AF.Sigmoid, [("pb", h)], ["A", ("pb", h)])
                    ts(A_[:, :], A_[:, :], oml[:, dr, j, hd:hd + 1], lbv[:, dr, j, hd:hd + 1], ALU.mult, ALU.add,
                       ["A", "oml", "lbv"], ["A"])
                    act(B_[:, :], A_[:, :], AF.Ln, ["A"], ["B"])
                    ts(A_[:, :], A_[:, :], -1.0, 1.0, ALU.mult, ALU.add, ["A"], ["A"])
                    S.add("dve", lambda e: e.tensor_tensor_scan(out=C_[:, :], data0=onesf[:, :], data1=B_[:, :],
                                                                initial=0.0, op0=ALU.mult, op1=ALU.add),
                          ["B", "onesf"], ["C"])
                    C3 = C_[:, :].rearrange("p (c t) -> p c t", c=16)
                    B3 = B_[:, :].rearrange("p (c t) -> p c t", c=16)
                    if phase == 0:
                        act(stg[dr][:, 128:129], C_[:, T - 1:T], AF.Exp, ["C"], [("stg", dr)])
                    if dr == 0:
                        memset(base[:, 0:1], 0.0, ["base"])
                        cp(base[:, 1:16], C3[:, 0:15, 63], ["C"], ["base"])
                        tt(C3, C3, base[:, :].unsqueeze(2).broadcast_to([P, 16, 64]), ALU.subtract, ["C", "base"], ["C"])
                        bb, bbk, other, otherk = C_, "C", B_, "B"
                        dcol = C3[:, :, 63]
                    else:
                        cp(base[:, :], C3[:, :, 63], ["C"], ["base"])
                        tt(B_[:, :], B_[:, :], C_[:, :], ALU.subtract, ["B", "C"], ["B"])
                        tt(B3, B3, base[:, :].unsqueeze(2).broadcast_to([P, 16, 64]), ALU.add, ["B", "base"], ["B"])
                        bb, bbk, other, otherk = B_, "B", C_, "C"
                        dcol = B3[:, :, 0]
                    act(Dc[:, dr, :], dcol, AF.Exp, [bbk], [("Dc", dr)])
                    if phase == 1:
                        act(other[:, :], bb[:, :], AF.Exp, [bbk, otherk], [otherk])
                        tt(qt[:, dr, :], qs[:, :], other[:, :], ALU.mult, ["qs", otherk], [("qt", dr)])
                    act(bb[:, :], bb[:, :], AF.Exp, [bbk], [bbk], scale=-1.0)
                    tt(kt[:, dr, :], A_[:, :], bb[:, :], ALU.mult, ["A", bbk], [("kt", dr)])
                    transposes(kt[:, dr, :], lambda g4, dr=dr: kT[:, dr, g4 * 4:(g4 + 1) * 4, :], [("kt", dr)], [("kT", dr)])
                    if phase == 0:
                        memset(Sst[:, dr, :], 0.0, [("Sst", dr)])
                    corder = list(range(16)) if dr == 0 else list(range(15, -1, -1))
                    if phase == 1:
                        cp(Sb[:, dr, corder[0], :], Sst[:, dr, :], [("Sst", dr)], [("Sb", dr, corder[0])], eng="act")
                    for ci, c in enumerate(corder):
                        p0 = (c % 2) * 64
                        mm(kvps[dr], kT[p0:p0 + 64, dr, c // 2, :], vT[p0:p0 + 64, c // 2, :], True, True,
                           [("kT", dr), "vT"], [("pb", 2 + dr)])
                        act(KD[dr][:, :], kvps[dr], AF.Copy, [("pb", 2 + dr), ("Dc", dr)], [("KD", dr), ("pb", 2 + dr)],
                            scale=Dc[:, dr, c:c + 1])
                        stt(Sst[:, dr, :], Sst[:, dr, :], Dc[:, dr, c:c + 1], KD[dr][:, :], ALU.mult, ALU.add,
                            [("Sst", dr), ("Dc", dr), ("KD", dr)], [("Sst", dr)])
                        if phase == 1 and ci < 15:
                            cn = corder[ci + 1]
                            cp(Sb[:, dr, cn, :], Sst[:, dr, :], [("Sst", dr)], [("Sb", dr, cn)], eng="act")
                    if phase == 0:
                        col0 = (hd * 2 + dr) * SROW
                        cp(stg[dr][:, 0:128], Sst[:, dr, :], [("Sst", dr)], [("stg", dr)])
                        dma(ags_in.ap()[:, col0:col0 + 129], stg[dr][:, 0:129], [("stg", dr)], ["ags_in"], "st1_%d" % dr)
                    else:
                        for g4 in range(2):
                            for bi in range(4):
                                blk = g4 * 4 + bi
                                mm(scp[:, bi * P:(bi + 1) * P], kt[:, dr, blk * P:(blk + 1) * P],
                                   qt[:, dr, blk * P:(blk + 1) * P], True, True, [("kt", dr), ("qt", dr)], [("pb", 5)])
                            tt(PT[:, g4 * 4:(g4 + 1) * 4, :], scp[:, :].rearrange("p (b t) -> p b t", b=4),
                               maskf[:, dr, :].unsqueeze(1).broadcast_to([P, 4, P]), ALU.mult, [("pb", 5), "maskf"],
                               [("PT", g4), ("pb", 5)])
                        first = {0: dr == 0, 1: dr == 0}
                        for blk in range(8):
                            bank = blk // 4
                            o_ = ob[bank]
                            st = first[bank]
                            first[bank] = False
                            mm(o_[:, (blk % 4) * P:(blk % 4 + 1) * P], vT[:, blk, :], PT[:, blk, :], st, False,
                               ["vT", ("PT", blk // 4)], [("pb", 6 + bank)])
                            for c in (2 * blk, 2 * blk + 1):
                                mm(o_[:, (c % 8) * 64:(c % 8 + 1) * 64], Sb[:, dr, c, :], qt[:, dr, c * 64:(c + 1) * 64],
                                   False, dr == 1 and c == 2 * blk + 1, [("Sb", dr, c), ("qt", dr)], [("pb", 6 + bank)])
                if phase == 1:
                    proj(4, hd, 0)
                    for h in range(2):
                        act(gs[:, h * 512:(h + 1) * 512], pb[h][:, :], AF.Silu, [("pb", h)], ["gs", ("pb", h)])
                    for h in range(2):
                        cp(A_[:, h * 512:(h + 1) * 512], ob[h][:, :], [("pb", 6 + h)], ["A", ("pb", 6 + h)], eng="act")
                        act(sqo[:, h * 512:(h + 1) * 512], A_[:, h * 512:(h + 1) * 512], AF.Square, ["A"], ["sqo"])
                        mm(pb[h][:, :], ones_h[:, :], sqo[:, h * 512:(h + 1) * 512], True, True, ["sqo", "ones_h"],
                           [("pb", h)])
                        rsqrt(B_[:, h * 512:(h + 1) * 512], pb[h][:, :], [("pb", h)], "B")
                    tt(A_[:, :], A_[:, :], B_[:, :], ALU.mult, ["A", "B"], ["A"])
                    stt(oT[:, hd, :], A_[:, :], gnorm[:, j:j + 1], gs[:, :], ALU.mult, ALU.mult, ["A", "gnorm", "gs"],
                        [("a", hd)])
            S.barrier()
            if phase == 0 and not os.environ.get("DBG_NOCC"):
                S.add("pool", lambda e: e.collective_compute("AllGather", ALU.bypass,
                                                              replica_groups=[list(range(NCORE))],
                                                              ins=[ags_in.ap().opt()], outs=[ags_out.ap().opt()]),
                      ["ags_in"], ["ags_out"])
        for q4 in range(4):
            outproj(DC, oT, q4 * 256, q4 * 256, lambda kc: W("hgrn_wout", j)[kc * P:(kc + 1) * P, :], 1, layer)

    def program():
        setup()
        for kind, l in plan:
            if kind == "dbg_state":
                S.add("pool", lambda e: e.collective_compute("AllGather", ALU.bypass,
                                                              replica_groups=[list(range(NCORE))],
                                                              ins=[ags_in.ap().opt()], outs=[ags_out.ap().opt()]),
                      ["ags_in"], ["ags_out"])
                dma(xh[:, :, :].rearrange("p a b -> p (a b)"), ags_out.ap()[0:P, 0:32], ["ags_out"], ["xh"], "dbgs")
                S.barrier()
            elif kind == "dbg_out":
                aa = actbuf[:, :].rearrange("p (j t) -> p j t", j=FC)
                memset(actbuf[:, 0:2048], 0.5, [("a", 0), ("a", 1)])
                S.barrier()
                outproj(2, aa, 0, 0, lambda kc: W("ffn_wout", 0)[kc * P:(kc + 1) * P, :], 3, 0)
            elif kind == "dbg_halo":
                halo_exchange()
            elif kind == "dbg_norm":
                halo_exchange()
                prenorm(2, 0, True)
            elif kind == "ffn":
                ffn(l)
            elif kind == "sconv":
                sconv(l, 2 * l + 1)
            else:
                hgrn(l, 2 * l)
        ov = out_d.ap().rearrange("(c p) t -> p c t", p=P)
        for c4 in range(4):
            dma(ov[:, c4 * 4:(c4 + 1) * 4, :], x[:, c4 * 4:(c4 + 1) * 4, :], [("x", c) for c in range(c4 * 4, c4 * 4 + 4)],
                ["out"], "st")

    S.plan = True
    program()
    S.plan = False
    program()
    S.emit(final_keys=["st"])
    nc._in_names = list(dram.keys())
    return nc


def prep(inputs):
    f32 = np.float32
    g = {k: np.asarray(v, dtype=f32) for k, v in inputs.items()}
    shared = {}
    w = g["hgrn_w_in"].reshape(2, 16, P, 80, P).transpose(0, 3, 2, 1, 4).reshape(2, 80, P, 2048)
    for l in range(2):
        shared["hgrn_win_%d" % l] = np.ascontiguousarray(w[l])
        shared["hgrn_wout_%d" % l] = g["hgrn_w_out"][l]
    w = g["sconv_w_in"].reshape(2, 16, P, 48, P).transpose(0, 3, 2, 1, 4).reshape(2, 48, P, 2048)
    for l in range(2):
        shared["sconv_win_%d" % l] = np.ascontiguousarray(w[l])
        shared["sconv_wout_%d" % l] = g["sconv_w_out"][l]
    w = g["ffn_w_in"].reshape(4, 16, P, 88, P).transpose(0, 3, 2, 1, 4).reshape(4, 88, P, 2048)
    for l in range(4):
        shared["ffn_win_%d" % l] = np.ascontiguousarray(w[l])
        shared["ffn_wout_%d" % l] = g["ffn_w_out"][l]
    nw = np.stack([g["norm_pre_mix"], g["norm_post_mix"], g["norm_pre_ffn"], g["norm_post_ffn"]])
    shared["normw"] = np.ascontiguousarray(nw.reshape(4, 4, 16, P).transpose(3, 0, 1, 2).reshape(P, 256))
    shared["fconv"] = np.ascontiguousarray(g["ffn_w_conv"].reshape(4, 3, 88, P).transpose(3, 0, 1, 2).reshape(P, -1))
    shared["sconvw"] = np.ascontiguousarray(g["sconv_w_conv"].reshape(2, 3, 16, P).transpose(3, 0, 1, 2).reshape(P, -1))
    shared["gnorm"] = np.ascontiguousarray(g["hgrn_gnorm"].T)
    shared["lbraw"] = np.ascontiguousarray(g["hgrn_lower_bounds"].reshape(2, 2, 16, P).transpose(3, 0, 1, 2).reshape(P, 64))
    maps = []
    for c in range(NCORE):
        b, s = c // 4, c % 4
        m = dict(shared)
        m["xT"] = np.ascontiguousarray(g["x"][b, s * T:(s + 1) * T, :].T)
        cm = np.zeros((P, 32), f32)
        for r in range(NCORE):
            same = (r // 4) == b
            cm[:, r] = 1.0 if (same and r == c - 1) else 0.0
            cm[:, 8 + r] = 1.0 if (same and r == c + 1) else 0.0
            cm[:, 16 + r] = 1.0 if (same and r < c) else 0.0
            cm[:, 24 + r] = 1.0 if (same and r > c) else 0.0
        m["cmask"] = cm
        maps.append(m)
    return maps


_NC_CACHE = {}


def run_plan(inputs, plan, maps=None):
    key = tuple(plan)
    if key not in _NC_CACHE:
        _NC_CACHE[key] = build(list(plan))
    nc = _NC_CACHE[key]
    if maps is None:
        maps = prep(inputs)
    x = np.asarray(inputs["x"], dtype=np.float32)
    in_maps = []
    for c, m in enumerate(maps):
        mm_ = {k: m[k] for k in nc._in_names if k != "xT"}
        b, s_ = c // 4, c % 4
        mm_["xT"] = np.ascontiguousarray(x[b, s_ * T:(s_ + 1) * T, :].T)
        in_maps.append(mm_)
    res = run_bass_kernel_spmd(nc, in_maps, core_ids=list(range(NCORE)))
    out = np.empty((2, 4096, D), np.float32)
    for c in range(NCORE):
        b, s_ = c // 4, c % 4
        out[b, s_ * T:(s_ + 1) * T, :] = res.results[c]["outT"].T
    return out


LAUNCH_GROUPS = [FULL_PLAN]


def kernel(**inputs):
    maps = prep(inputs)
    cur = dict(inputs)
    out = None
    for grp in LAUNCH_GROUPS:
        out = run_plan(cur, grp, maps)
        cur["x"] = out
    return out
```
